# Optimizing a Trainium2 kernel written in Bass

```python
import math
import jax, jax.numpy as jnp
from jax import lax
import numpy as np

D_MODEL = 1024
BATCH = 4
SEQ = 4096
DEPTH = 2

D_MIX = D_MODEL
ATTN_WIDTH = 3 * D_MIX // 8
LRU_WIDTH = 3 * D_MIX // 8
S5_WIDTH = D_MIX - ATTN_WIDTH - LRU_WIDTH

HEAD_DIM = 64
N_ATTN_HEADS = ATTN_WIDTH // HEAD_DIM
DILATED_PAIRS = ((128, 1), (512, 4), (2048, 16))
ATTN_BLOCK = 128
ROPE_THETA = 10000.0

LRU_HEAD = 64
N_LRU_HEADS = LRU_WIDTH // LRU_HEAD
LRU_CONV = 4
LRU_C = 8.0

S5_GROUP = 16
N_S5_GROUPS = S5_WIDTH // S5_GROUP
S5_STATE = 64

D_FF = 3 * D_MODEL
FFN_CONV = 3

DEEPNORM_ALPHA = (2 * DEPTH) ** 0.25
DEEPNORM_BETA = (8 * DEPTH) ** -0.25
LN_EPS = 1e-5
RMS_EPS = 1e-6

Q_OFF = 0
K_OFF = ATTN_WIDTH
V_OFF = 2 * ATTN_WIDTH
LRU_X_OFF = 3 * ATTN_WIDTH
LRU_G_OFF = LRU_X_OFF + LRU_WIDTH
S5_OFF = LRU_G_OFF + LRU_WIDTH
D_IN = S5_OFF + S5_WIDTH

kernel_name = 'hybrid_dilated_attn_rglru_s5_deepnorm'


def _layer_norm(x, g, b):
    xf = x.astype(jnp.float32)
    mu = jnp.mean(xf, axis=-1, keepdims=True)
    var = jnp.mean(jnp.square(xf - mu), axis=-1, keepdims=True)
    y = (xf - mu) * lax.rsqrt(var + LN_EPS) * g.astype(jnp.float32) + b.astype(jnp.float32)
    return y.astype(x.dtype)


def _rms_norm(x, g):
    xf = x.astype(jnp.float32)
    ms = jnp.mean(jnp.square(xf), axis=-1, keepdims=True)
    return xf * lax.rsqrt(ms + RMS_EPS) * g.astype(jnp.float32)


def _causal_dwconv(x, w, b):
    k = w.shape[0]
    y = lax.conv_general_dilated(
        x, w[:, None, :].astype(x.dtype), window_strides=(1,), padding=((k - 1, 0),),
        dimension_numbers=('NWC', 'WIO', 'NWC'), feature_group_count=x.shape[-1])
    return y + b.astype(x.dtype)


def _rope(x):
    s = x.shape[1]
    half = HEAD_DIM // 2
    pos = jnp.arange(s, dtype=jnp.float32)
    inv = ROPE_THETA ** (-jnp.arange(half, dtype=jnp.float32) * 2.0 / HEAD_DIM)
    ang = pos[:, None] * inv[None, :]
    cos = jnp.cos(ang)[None, :, None, :]
    sin = jnp.sin(ang)[None, :, None, :]
    xf = x.astype(jnp.float32)
    x1, x2 = xf[..., :half], xf[..., half:]
    return jnp.concatenate([x1 * cos - x2 * sin, x2 * cos + x1 * sin], axis=-1)


def _dilated_branch(q, k, v, window, dilation):
    b, s, nh, hd = q.shape
    m = s // dilation
    nback = window // dilation
    nb = -(-m // ATTN_BLOCK)
    mp = nb * ATTN_BLOCK

    def strided(t, left):
        t = t.reshape(b, m, dilation, nh, hd)
        return jnp.pad(t, ((0, 0), (left, mp - m), (0, 0), (0, 0), (0, 0)))

    def key_blocks(t):
        t = strided(t, ATTN_BLOCK).reshape(b, nb + 1, ATTN_BLOCK, dilation, nh, hd)
        return jnp.concatenate([t[:, :-1], t[:, 1:]], axis=2)

    qb = strided(q, 0).reshape(b, nb, ATTN_BLOCK, dilation, nh, hd)
    kb = key_blocks(k)
    vb = key_blocks(v)
    scores = jnp.einsum('bnqchd,bnkchd->bnchqk', qb, kb) * (hd ** -0.5)
    qi = jnp.arange(ATTN_BLOCK)[:, None]
    ki = jnp.arange(2 * ATTN_BLOCK)[None, :]
    dist = qi + ATTN_BLOCK - ki
    blk = jnp.arange(nb)[:, None, None]
    valid = (dist >= 0) & (dist <= nback) & ((blk - 1) * ATTN_BLOCK + ki >= 0)
    scores = jnp.where(valid[None, :, None, None], scores, -jnp.inf)
    lse = jax.nn.logsumexp(scores, axis=-1)
    probs = jnp.exp(scores - lse[..., None])
    out = jnp.einsum('bnchqk,bnkchd->bnqchd', probs, vb)
    out = out.reshape(b, mp, dilation, nh, hd)[:, :m].reshape(b, s, nh, hd)
    lse = jnp.transpose(lse, (0, 1, 4, 2, 3)).reshape(b, mp, dilation, nh)[:, :m].reshape(b, s, nh)
    return out, lse


def _dilated_attention(q, k, v):
    q = _rope(q)
    k = _rope(k)
    v = v.astype(jnp.float32)
    outs, lses = [], []
    for window, dilation in DILATED_PAIRS:
        o, l = _dilated_branch(q, k, v, window, dilation)
        outs.append(o)
        lses.append(l)
    wts = jax.nn.softmax(jnp.stack(lses, axis=0), axis=0)
    return jnp.sum(wts[..., None] * jnp.stack(outs, axis=0), axis=0)


def _linear_combine(e1, e2):
    a1, b1 = e1
    a2, b2 = e2
    return (a1 * a2, a2 * b1 + b2)


def _complex_combine(e1, e2):
    ar1, ai1, br1, bi1 = e1
    ar2, ai2, br2, bi2 = e2
    return (ar2 * ar1 - ai2 * ai1,
            ar2 * ai1 + ai2 * ar1,
            ar2 * br1 - ai2 * bi1 + br2,
            ar2 * bi1 + ai2 * br1 + bi2)


def _rg_lru_branch(xr, gate, conv_w, conv_b, wr, br, wi, bi, lam):
    b, s, _ = xr.shape
    f32 = jnp.float32
    xc = _causal_dwconv(xr, conv_w, conv_b).astype(f32)
    xh = xc.reshape(b, s, N_LRU_HEADS, LRU_HEAD)
    r = jax.nn.sigmoid(jnp.einsum('bshi,hij->bshj', xh, wr.astype(f32)).reshape(b, s, LRU_WIDTH) + br.astype(f32))
    i = jax.nn.sigmoid(jnp.einsum('bshi,hij->bshj', xh, wi.astype(f32)).reshape(b, s, LRU_WIDTH) + bi.astype(f32))
    log_a = -LRU_C * r * jax.nn.softplus(-lam.astype(f32))
    a = jnp.exp(log_a)
    u = jnp.sqrt(-jnp.expm1(2.0 * log_a)) * (i * xc)
    _, h = lax.associative_scan(_linear_combine, (a, u), axis=1)
    return h * jax.nn.gelu(gate.astype(f32))


def _s5_branch(u, a_re, a_im, b_re, b_im, c_re, c_im, d, log_step, w_glu, b_glu):
    f32 = jnp.float32
    bsz, s, _ = u.shape
    uf = u.astype(f32).reshape(bsz, s, N_S5_GROUPS, S5_GROUP)
    a_re, a_im = a_re.astype(f32), a_im.astype(f32)
    b_re, b_im = b_re.astype(f32), b_im.astype(f32)
    step = jnp.exp(log_step.astype(f32))[:, None]
    dt_re, dt_im = step * a_re, step * a_im
    mag = jnp.exp(dt_re)
    ab_re, ab_im = mag * jnp.cos(dt_im), mag * jnp.sin(dt_im)
    z_re, z_im = ab_re - 1.0, ab_im
    den = a_re * a_re + a_im * a_im
    f_re = (z_re * a_re + z_im * a_im) / den
    f_im = (z_im * a_re - z_re * a_im) / den
    bb_re = f_re[..., None] * b_re - f_im[..., None] * b_im
    bb_im = f_re[..., None] * b_im + f_im[..., None] * b_re
    bu_re = jnp.einsum('bsgc,gpc->bsgp', uf, bb_re)
    bu_im = jnp.einsum('bsgc,gpc->bsgp', uf, bb_im)
    shape = bu_re.shape
    elems = (jnp.broadcast_to(ab_re, shape), jnp.broadcast_to(ab_im, shape), bu_re, bu_im)
    _, _, x_re, x_im = lax.associative_scan(_complex_combine, elems, axis=1)
    y = (jnp.einsum('bsgp,gcp->bsgc', x_re, c_re.astype(f32))
         - jnp.einsum('bsgp,gcp->bsgc', x_im, c_im.astype(f32))
         + d.astype(f32) * uf)
    y = jax.nn.gelu(y.reshape(bsz, s, S5_WIDTH))
    return y * jax.nn.sigmoid(y @ w_glu.astype(f32) + b_glu.astype(f32))


def _hybrid_mixer(h, w_in, lru_conv_w, lru_conv_b, lru_wr, lru_br, lru_wi, lru_bi, lru_lambda,
                  s5_a_re, s5_a_im, s5_b_re, s5_b_im, s5_c_re, s5_c_im, s5_d, s5_log_step,
                  s5_w_glu, s5_b_glu, mix_norm_g, w_out):
    b, s, _ = h.shape
    proj = h @ w_in

    def heads(off):
        return proj[..., off:off + ATTN_WIDTH].reshape(b, s, N_ATTN_HEADS, HEAD_DIM)

    attn = _dilated_attention(heads(Q_OFF), heads(K_OFF), heads(V_OFF)).reshape(b, s, ATTN_WIDTH)
    lru = _rg_lru_branch(proj[..., LRU_X_OFF:LRU_X_OFF + LRU_WIDTH],
                         proj[..., LRU_G_OFF:LRU_G_OFF + LRU_WIDTH],
                         lru_conv_w, lru_conv_b, lru_wr, lru_br, lru_wi, lru_bi, lru_lambda)
    ssm = _s5_branch(proj[..., S5_OFF:S5_OFF + S5_WIDTH], s5_a_re, s5_a_im, s5_b_re, s5_b_im,
                     s5_c_re, s5_c_im, s5_d, s5_log_step, s5_w_glu, s5_b_glu)
    g_attn = mix_norm_g[:ATTN_WIDTH]
    g_lru = mix_norm_g[ATTN_WIDTH:ATTN_WIDTH + LRU_WIDTH]
    g_s5 = mix_norm_g[ATTN_WIDTH + LRU_WIDTH:]
    mixed = jnp.concatenate([_rms_norm(attn, g_attn), _rms_norm(lru, g_lru), _rms_norm(ssm, g_s5)],
                            axis=-1).astype(h.dtype)
    return mixed @ w_out


def _conv_ffn(h, w_up, conv_w, conv_b, w_down):
    up = _causal_dwconv(h @ w_up, conv_w, conv_b)
    gate, val = jnp.split(up, 2, axis=-1)
    return (jax.nn.gelu(gate) * val) @ w_down


def setup_inputs(seed: int = 0) -> dict:
    key = jax.random.key(seed)
    ks = iter(jax.random.split(key, 32))
    f32 = jnp.float32
    L = DEPTH

    def nrm(shape, scale):
        return jax.random.normal(next(ks), shape, f32) * scale

    x = nrm((BATCH, SEQ, D_MODEL), 1.0)
    w_in = nrm((L, D_MODEL, D_IN), D_MODEL ** -0.5)
    lru_conv_w = nrm((L, LRU_CONV, LRU_WIDTH), LRU_CONV ** -0.5)
    lru_conv_b = nrm((L, LRU_WIDTH), 0.02)
    lru_wr = nrm((L, N_LRU_HEADS, LRU_HEAD, LRU_HEAD), LRU_HEAD ** -0.5)
    lru_br = nrm((L, LRU_WIDTH), 0.02)
    lru_wi = nrm((L, N_LRU_HEADS, LRU_HEAD, LRU_HEAD), LRU_HEAD ** -0.5)
    lru_bi = nrm((L, LRU_WIDTH), 0.02)
    a_c = jax.random.uniform(next(ks), (L, LRU_WIDTH), f32, 0.9, 0.999)
    a0 = a_c ** (1.0 / LRU_C)
    lru_lambda = jnp.log(a0) - jnp.log1p(-a0)
    s5_a_re = -0.5 + nrm((L, N_S5_GROUPS, S5_STATE), 0.01)
    s5_a_im = jnp.pi * jnp.arange(S5_STATE, dtype=f32) + nrm((L, N_S5_GROUPS, S5_STATE), 0.01)
    s5_b_re = nrm((L, N_S5_GROUPS, S5_STATE, S5_GROUP), (2 * S5_GROUP) ** -0.5)
    s5_b_im = nrm((L, N_S5_GROUPS, S5_STATE, S5_GROUP), (2 * S5_GROUP) ** -0.5)
    s5_c_re = nrm((L, N_S5_GROUPS, S5_GROUP, S5_STATE), (2 * S5_STATE) ** -0.5)
    s5_c_im = nrm((L, N_S5_GROUPS, S5_GROUP, S5_STATE), (2 * S5_STATE) ** -0.5)
    s5_d = nrm((L, N_S5_GROUPS, S5_GROUP), 1.0)
    s5_log_step = jax.random.uniform(next(ks), (L, N_S5_GROUPS), f32, math.log(1e-3), math.log(1e-1))
    s5_w_glu = nrm((L, S5_WIDTH, S5_WIDTH), S5_WIDTH ** -0.5)
    s5_b_glu = nrm((L, S5_WIDTH), 0.02)
    mix_norm_g = 1.0 + nrm((L, D_MIX), 0.02)
    w_out = nrm((L, D_MIX, D_MODEL), D_MIX ** -0.5 * DEEPNORM_BETA)
    ln1_g = 1.0 + nrm((L, D_MODEL), 0.02)
    ln1_b = nrm((L, D_MODEL), 0.02)
    w_up = nrm((L, D_MODEL, 2 * D_FF), D_MODEL ** -0.5)
    ffn_conv_w = nrm((L, FFN_CONV, 2 * D_FF), FFN_CONV ** -0.5)
    ffn_conv_b = nrm((L, 2 * D_FF), 0.02)
    w_down = nrm((L, D_FF, D_MODEL), D_FF ** -0.5 * DEEPNORM_BETA)
    ln2_g = 1.0 + nrm((L, D_MODEL), 0.02)
    ln2_b = nrm((L, D_MODEL), 0.02)
    return {'x': x, 'w_in': w_in, 'lru_conv_w': lru_conv_w, 'lru_conv_b': lru_conv_b,
            'lru_wr': lru_wr, 'lru_br': lru_br, 'lru_wi': lru_wi, 'lru_bi': lru_bi,
            'lru_lambda': lru_lambda, 's5_a_re': s5_a_re, 's5_a_im': s5_a_im,
            's5_b_re': s5_b_re, 's5_b_im': s5_b_im, 's5_c_re': s5_c_re, 's5_c_im': s5_c_im,
            's5_d': s5_d, 's5_log_step': s5_log_step, 's5_w_glu': s5_w_glu, 's5_b_glu': s5_b_glu,
            'mix_norm_g': mix_norm_g, 'w_out': w_out, 'ln1_g': ln1_g, 'ln1_b': ln1_b,
            'w_up': w_up, 'ffn_conv_w': ffn_conv_w, 'ffn_conv_b': ffn_conv_b, 'w_down': w_down,
            'ln2_g': ln2_g, 'ln2_b': ln2_b}


def reference(x, w_in, lru_conv_w, lru_conv_b, lru_wr, lru_br, lru_wi, lru_bi, lru_lambda,
              s5_a_re, s5_a_im, s5_b_re, s5_b_im, s5_c_re, s5_c_im, s5_d, s5_log_step,
              s5_w_glu, s5_b_glu, mix_norm_g, w_out, ln1_g, ln1_b, w_up, ffn_conv_w, ffn_conv_b,
              w_down, ln2_g, ln2_b):
    h = x
    for l in range(DEPTH):
        mix = _hybrid_mixer(h, w_in[l], lru_conv_w[l], lru_conv_b[l], lru_wr[l], lru_br[l],
                            lru_wi[l], lru_bi[l], lru_lambda[l], s5_a_re[l], s5_a_im[l],
                            s5_b_re[l], s5_b_im[l], s5_c_re[l], s5_c_im[l], s5_d[l],
                            s5_log_step[l], s5_w_glu[l], s5_b_glu[l], mix_norm_g[l], w_out[l])
        h = _layer_norm(DEEPNORM_ALPHA * h + mix, ln1_g[l], ln1_b[l])
        ffn = _conv_ffn(h, w_up[l], ffn_conv_w[l], ffn_conv_b[l], w_down[l])
        h = _layer_norm(DEEPNORM_ALPHA * h + ffn, ln2_g[l], ln2_b[l])
    return h
```

```python
import numpy as np
import concourse.bass as bass
import concourse.mybir as mybir
from concourse.bass_utils import run_bass_kernel_spmd

F32 = mybir.dt.float32
BF16 = mybir.dt.bfloat16
AF = mybir.ActivationFunctionType
ALU = mybir.AluOpType

D_MODEL = 1024
BATCH = 4
SEQ = 4096
DEPTH = 2
D_FF = 3072
ALPHA = float((2 * DEPTH) ** 0.25)
LN_EPS = 1e-5
RMS_EPS = 1e-6
N_CORES = 8


class Sync:
    def __init__(self, nc, ndma=32):
        self.nc = nc
        self.eng = {"pe": nc.tensor, "dve": nc.vector, "act": nc.scalar, "pool": nc.gpsimd, "sp": nc.sync}
        self.sem = {k: nc.alloc_semaphore(name=f"prog_{k}") for k in self.eng}
        self.cnt = {k: 0 for k in self.eng}
        self.seen = {k: {j: 0 for j in self.eng} for k in self.eng}
        self.seen_dma = {k: {} for k in self.eng}
        self.lastw = {}
        self.readers = {}
        self.ndma = ndma
        self.dsem = [nc.alloc_semaphore(name=f"dma_{i}") for i in range(ndma)]
        self.dcnt = [0] * ndma
        self.dnext = 0

    def _wait(self, e, tok):
        if tok[0] == "eng":
            _, pe, c = tok
            if pe == e and e == "pe":
                return
            if self.seen[e][pe] >= c:
                return
            self.eng[e].wait_ge(self.sem[pe], c)
            self.seen[e][pe] = c
        else:
            _, si, val = tok
            if self.seen_dma[e].get(si, 0) >= val:
                return
            self.eng[e].wait_ge(self.dsem[si], val)
            self.seen_dma[e][si] = val

    def _deps(self, e, reads, writes):
        for r in reads:
            t = self.lastw.get(r)
            if t is not None:
                self._wait(e, t)
        for w in writes:
            t = self.lastw.get(w)
            if t is not None:
                self._wait(e, t)
            for t in self.readers.get(w, ()):
                self._wait(e, t)

    def _commit(self, tok, reads, writes):
        for w in writes:
            self.lastw[w] = tok
            self.readers[w] = []
        for r in reads:
            if r not in writes:
                lst = self.readers.setdefault(r, [])
                lst[:] = [t for t in lst if not (t[0] == tok[0] and t[1] == tok[1])]
                lst.append(tok)

    def op(self, e, fn, reads=(), writes=()):
        self._deps(e, reads, writes)
        inst = fn(self.eng[e])
        self.cnt[e] += 1
        inst.then_inc(self.sem[e], 1)
        self._commit(("eng", e, self.cnt[e]), reads, writes)
        return inst

    def dma(self, q, out, in_, reads=(), writes=(), **kw):
        si = self.dnext
        self.dnext = (self.dnext + 1) % self.ndma
        if self.dcnt[si] > 0:
            self._wait(q, ("dma", si, self.dcnt[si]))
        self._deps(q, reads, writes)
        inst = self.eng[q].dma_start(out=out, in_=in_, **kw)
        self.dcnt[si] += 16
        inst.then_inc(self.dsem[si], 16)
        self._commit(("dma", si, self.dcnt[si]), reads, writes)
        return inst

    def finish(self, e, resources):
        for r in resources:
            t = self.lastw.get(r)
            if t is not None:
                self._wait(e, t)


TOKB = 2048
CWB = 256
WB = CWB + 2
PB_GMIX = 0
PB_BGLU = 8
PB_LN1G = 10
PB_LN1B = 18
PB_LN2G = 26
PB_LN2B = 34
PB_CW = 42
PB_CB = 42 + 144
PB_FLAG = PB_CB + 48
NPB = PB_FLAG + 1


def build_ffn():
    nc = bass.Bass("TRN2", target_bir_lowering=False)
    S = Sync(nc)
    hT = nc.dram_tensor("hT", [1024, TOKB + 2], F32, kind="ExternalInput").ap()
    mixT = nc.dram_tensor("mixT", [1024, TOKB + 2], F32, kind="ExternalInput").ap()
    w_glu = nc.dram_tensor("w_glu", [256, 256], F32, kind="ExternalInput").ap()
    w_out = nc.dram_tensor("w_out", [1024, 1024], F32, kind="ExternalInput").ap()
    w_up = nc.dram_tensor("w_up", [1024, 6144], F32, kind="ExternalInput").ap()
    w_down = nc.dram_tensor("w_down", [3072, 1024], F32, kind="ExternalInput").ap()
    pvec = nc.dram_tensor("pvec", [128, NPB], F32, kind="ExternalInput").ap()
    outT = nc.dram_tensor("outT", [1024, TOKB], F32, kind="ExternalOutput").ap()

    A = nc.alloc_sbuf_tensor
    pv = A("pv", [128, NPB], F32)
    wglu_sb = A("wglu_sb", [128, 2, 256], BF16)
    wout_sb = A("wout_sb", [128, 8, 1024], BF16)
    wup_sb = A("wup_sb", [128, 8, 6144], BF16)
    wdn_sb = A("wdn_sb", [128, 24, 1024], BF16)
    onesF = A("onesF", [128, 128], F32)
    Rb = A("Rb", [128, 8, WB], F32)
    Xb = A("Xb", [128, 3, WB], F32)
    Gb = A("Gb", [128, 2, WB], F32)
    Gbf = A("Gbf", [128, 2, WB], BF16)
    sqt = A("sqt", [128, 2, WB], F32)
    rs = A("rs", [128, WB], F32)
    mean_s = A("mean_s", [128, WB], F32)
    m2 = A("m2", [128, WB], F32)
    tA = A("tA", [128, 2, WB], F32)
    mixed = A("mixed", [128, 8, WB], BF16)
    h1T = mixed
    act = A("act", [128, 24, CWB], BF16)
    cg = A("cg", [128, 2, CWB], F32)
    cv = A("cv", [128, 2, CWB], F32)
    P = nc.alloc_psum_tensor
    pup = [P(f"pup{i}", [128, 512], F32) for i in range(4)]
    pgen = [P(f"pgen{i}", [128, 512], F32) for i in range(2)]
    pst = [P(f"pst{i}", [128, 512], F32) for i in range(2)]

    S.dma("sp", pv[:], pvec, writes=["pv"])
    S.op("dve", lambda e: e.memset(onesF[:], 1.0), writes=["onesF"])
    S.dma("pool", wglu_sb[:], w_glu.rearrange("(k p) n -> p k n", p=128), writes=["wglu"])
    for k in range(8):
        S.dma("pool", wout_sb[:, k, :], w_out[k * 128:(k + 1) * 128, :], writes=[f"wout{k}"], max_dma_last_dim=4096)
    for k in range(8):
        for hh in range(2):
            S.dma("pool", wup_sb[:, k, hh * 3072:(hh + 1) * 3072], w_up[k * 128:(k + 1) * 128, hh * 3072:(hh + 1) * 3072],
                  writes=[f"wup{k}_{hh}"], max_dma_last_dim=4096)
    for k in range(24):
        S.dma("pool", wdn_sb[:, k, :], w_down[k * 128:(k + 1) * 128, :], writes=[f"wdn{k}"], max_dma_last_dim=4096)
    WOUT = [f"wout{k}" for k in range(8)]
    WUP = [f"wup{k}_{hh}" for k in range(8) for hh in range(2)]
    WDN = [f"wdn{k}" for k in range(24)]

    hT3 = hT.rearrange("(k p) w -> p k w", p=128)
    mixT3 = mixT.rearrange("(k p) w -> p k w", p=128)
    outT3 = outT.rearrange("(k p) w -> p k w", p=128)

    gen_i = [0]

    def next_pgen():
        i = gen_i[0] % 2
        gen_i[0] += 1
        return pgen[i], f"pgen{i}"

    def layer_norm(buf0, w, gcol, bcol, bf_out=None, off=0):
        name = buf0.name

        class _V:
            def __getitem__(self, idx):
                p, f, sl = idx
                return buf0[p, f, off + sl.start:off + sl.stop]
        buf = _V()
        for f in range(8):
            S.op("act", lambda e, f=f: e.activation(out=sqt[:, f % 2, 0:w], in_=buf[:, f, 0:w], func=AF.Square),
                 reads=[name], writes=[f"sqt{f % 2}"])
            S.op("pe", lambda e, f=f: e.matmul(pst[0][:, 0:w], onesF[:], buf[:, f, 0:w], start=(f == 0), stop=(f == 7)),
                 reads=[name, "onesF"], writes=["pst0"])
            S.op("pe", lambda e, f=f: e.matmul(pst[1][:, 0:w], onesF[:], sqt[:, f % 2, 0:w], start=(f == 0), stop=(f == 7)),
                 reads=[f"sqt{f % 2}", "onesF"], writes=["pst1"])
        S.op("act", lambda e: e.activation(out=mean_s[:, 0:w], in_=pst[0][:, 0:w], func=AF.Copy, scale=1.0 / 1024),
             reads=["pst0"], writes=["mean_s"])
        S.op("dve", lambda e: e.tensor_tensor(out=m2[:, 0:w], in0=mean_s[:, 0:w], in1=mean_s[:, 0:w], op=ALU.mult),
             reads=["mean_s"], writes=["m2"])
        S.op("dve", lambda e: e.scalar_tensor_tensor(out=m2[:, 0:w], in0=pst[1][:, 0:w], scalar=1.0 / 1024, in1=m2[:, 0:w],
                                                     op0=ALU.mult, op1=ALU.subtract),
             reads=["pst1", "m2"], writes=["m2"])
        S.op("act", lambda e: e.activation(out=rs[:, 0:w], in_=m2[:, 0:w], func=AF.Sqrt, bias=LN_EPS, scale=1.0),
             reads=["m2"], writes=["rs"])
        S.op("dve", lambda e: e.reciprocal(out=rs[:, 0:w], in_=rs[:, 0:w]), reads=["rs"], writes=["rs"])
        for f in range(8):
            t = tA[:, f % 2, 0:w]
            tn = f"tA{f % 2}"
            S.op("dve", lambda e, f=f, t=t: e.tensor_tensor(out=t, in0=buf[:, f, 0:w], in1=mean_s[:, 0:w], op=ALU.subtract),
                 reads=[name, "mean_s"], writes=[tn])
            S.op("pool", lambda e, f=f, t=t: e.tensor_tensor(out=t, in0=t, in1=rs[:, 0:w], op=ALU.mult),
                 reads=[tn, "rs"], writes=[tn])
            S.op("dve", lambda e, f=f, t=t: e.tensor_scalar(out=buf[:, f, 0:w], in0=t, scalar1=pv[:, gcol + f:gcol + f + 1],
                                                            scalar2=pv[:, bcol + f:bcol + f + 1], op0=ALU.mult, op1=ALU.add),
                 reads=[tn, "pv"], writes=[name])
            if bf_out is not None:
                S.op("act", lambda e, f=f: e.activation(out=bf_out[:, f, 0:w], in_=buf[:, f, 0:w], func=AF.Copy),
                     reads=[name], writes=[f"mixed{f}"])

    NCH = TOKB // CWB
    for c in range(NCH):
        c0 = c * CWB
        S.dma("sp", Rb[:], hT3[:, :, c0:c0 + WB], writes=["Rb"])
        S.dma("sp", Gb[:], mixT3[:, 6:8, c0:c0 + WB], writes=["Gb"])
        S.op("act", lambda e: e.activation(out=Gbf[:], in_=Gb[:], func=AF.Copy), reads=["Gb"], writes=["Gbf"])
        for m in range(2):
            pz, pzn = next_pgen()
            for k in range(2):
                S.op("pe", lambda e, k=k, m=m, pz=pz: e.matmul(pz[:, 0:WB], wglu_sb[:, k, m * 128:(m + 1) * 128], Gbf[:, k, :],
                                                               start=(k == 0), stop=(k == 1)),
                     reads=["Gbf", "wglu"], writes=[pzn])
            S.op("act", lambda e, m=m, pz=pz: e.activation(out=tA[:, m, :], in_=pz[:, 0:WB], func=AF.Sigmoid,
                                                           bias=pv[:, PB_BGLU + m:PB_BGLU + m + 1], scale=1.0),
                 reads=[pzn, "pv"], writes=[f"tA{m}"])
            S.op("dve", lambda e, m=m: e.tensor_tensor(out=Gb[:, m, :], in0=Gb[:, m, :], in1=tA[:, m, :], op=ALU.mult),
                 reads=[f"tA{m}", "Gb"], writes=["Gb"])
        for mi, (tiles, C) in enumerate([([0, 1, 2], 384), ([3, 4, 5], 384), ([6, 7], 256)]):
            if mi < 2:
                S.dma("sp", Xb[:], mixT3[:, tiles[0]:tiles[0] + 3, c0:c0 + WB], writes=["Xb"])
                X, Xn = Xb, "Xb"
            else:
                X, Xn = Gb, "Gb"
            pss, pssn = next_pgen()
            nt = len(tiles)
            for i in range(nt):
                S.op("act", lambda e, i=i, X=X: e.activation(out=sqt[:, i % 2, :], in_=X[:, i, :], func=AF.Square),
                     reads=[Xn], writes=[f"sqt{i % 2}"])
                S.op("pe", lambda e, i=i, pss=pss, nt=nt: e.matmul(pss[:, 0:WB], onesF[:], sqt[:, i % 2, :], start=(i == 0), stop=(i == nt - 1)),
                     reads=[f"sqt{i % 2}", "onesF"], writes=[pssn])
            S.op("act", lambda e, pss=pss, C=C: e.activation(out=rs[:], in_=pss[:, 0:WB], func=AF.Sqrt, bias=RMS_EPS, scale=1.0 / C),
                 reads=[pssn], writes=["rs"])
            S.op("dve", lambda e: e.reciprocal(out=rs[:], in_=rs[:]), reads=["rs"], writes=["rs"])
            for i, k in enumerate(tiles):
                S.op("dve", lambda e, i=i, k=k, X=X: e.scalar_tensor_tensor(out=mixed[:, k, :], in0=X[:, i, :],
                                                                            scalar=pv[:, PB_GMIX + k:PB_GMIX + k + 1], in1=rs[:],
                                                                            op0=ALU.mult, op1=ALU.mult),
                     reads=[Xn, "rs", "pv"], writes=[f"mixed{k}"])
        for f in range(8):
            po, pon = next_pgen()
            for k in range(8):
                S.op("pe", lambda e, k=k, f=f, po=po: e.matmul(po[:, 0:WB], wout_sb[:, k, f * 128:(f + 1) * 128], mixed[:, k, :],
                                                               start=(k == 0), stop=(k == 7)),
                     reads=[f"mixed{k}", f"wout{k}"], writes=[pon])
            S.op("dve", lambda e, f=f, po=po: e.scalar_tensor_tensor(out=Rb[:, f, :], in0=Rb[:, f, :], scalar=ALPHA, in1=po[:, 0:WB],
                                                                     op0=ALU.mult, op1=ALU.add),
                 reads=[pon, "Rb"], writes=["Rb"])
        layer_norm(Rb, WB, PB_LN1G, PB_LN1B, bf_out=h1T)
        if c == 0:
            for f in range(8):
                S.op("dve", lambda e, f=f: e.tensor_scalar(out=h1T[:, f, 0:2], in0=h1T[:, f, 0:2], scalar1=pv[:, PB_FLAG:PB_FLAG + 1],
                                                           scalar2=None, op0=ALU.mult),
                     reads=[f"mixed{f}", "pv"], writes=[f"mixed{f}"])
        for j in range(24):
            b2 = j % 2
            pg, pgn = pup[2 * b2], f"pup{2 * b2}"
            pvv, pvn = pup[2 * b2 + 1], f"pup{2 * b2 + 1}"
            for (pt, ptn, col0) in ((pg, pgn, j * 128), (pvv, pvn, 3072 + j * 128)):
                hh = 0 if col0 < 3072 else 1
                for k in range(8):
                    S.op("pe", lambda e, k=k, pt=pt, col0=col0: e.matmul(pt[:, 0:WB], wup_sb[:, k, col0:col0 + 128], h1T[:, k, :],
                                                                         start=(k == 0), stop=(k == 7)),
                         reads=[f"mixed{k}", f"wup{k}_{hh}"], writes=[ptn])
            for (pt, ptn, jt, dst, dn) in ((pg, pgn, j, cg, f"cg{b2}"), (pvv, pvn, 24 + j, cv, f"cv{b2}")):
                d = dst[:, b2, :]
                S.op("act", lambda e, pt=pt, jt=jt, d=d: e.activation(out=d, in_=pt[:, 2:2 + CWB], func=AF.Identity,
                                                                      bias=pv[:, PB_CB + jt:PB_CB + jt + 1],
                                                                      scale=pv[:, PB_CW + 2 * 48 + jt:PB_CW + 2 * 48 + jt + 1]),
                     reads=[ptn, "pv"], writes=[dn])
                S.op("dve", lambda e, pt=pt, jt=jt, d=d: e.scalar_tensor_tensor(out=d, in0=pt[:, 1:1 + CWB],
                                                                                scalar=pv[:, PB_CW + 48 + jt:PB_CW + 48 + jt + 1], in1=d,
                                                                                op0=ALU.mult, op1=ALU.add),
                     reads=[ptn, dn, "pv"], writes=[dn])
                S.op("dve", lambda e, pt=pt, jt=jt, d=d: e.scalar_tensor_tensor(out=d, in0=pt[:, 0:CWB],
                                                                                scalar=pv[:, PB_CW + jt:PB_CW + jt + 1], in1=d,
                                                                                op0=ALU.mult, op1=ALU.add),
                     reads=[ptn, dn, "pv"], writes=[dn])
            S.op("act", lambda e, b2=b2: e.activation(out=cg[:, b2, :], in_=cg[:, b2, :], func=AF.Gelu_apprx_tanh),
                 reads=[f"cg{b2}"], writes=[f"cg{b2}"])
            S.op("pool", lambda e, b2=b2, j=j: e.tensor_tensor(out=act[:, j, :], in0=cg[:, b2, :], in1=cv[:, b2, :], op=ALU.mult),
                 reads=[f"cg{b2}", f"cv{b2}"], writes=[f"act{j}"])
        for f in range(8):
            po, pon = next_pgen()
            for k in range(24):
                S.op("pe", lambda e, k=k, f=f, po=po: e.matmul(po[:, 0:CWB], wdn_sb[:, k, f * 128:(f + 1) * 128], act[:, k, :],
                                                               start=(k == 0), stop=(k == 23)),
                     reads=[f"act{k}", f"wdn{k}"], writes=[pon])
            S.op("dve", lambda e, f=f, po=po: e.scalar_tensor_tensor(out=Rb[:, f, 2:2 + CWB], in0=Rb[:, f, 2:2 + CWB], scalar=ALPHA, in1=po[:, 0:CWB],
                                                                     op0=ALU.mult, op1=ALU.add),
                 reads=[pon, "Rb"], writes=["Rb"])
        layer_norm(Rb, CWB, PB_LN2G, PB_LN2B, off=2)
        S.dma("sp", outT3[:, :, c0:c0 + CWB], Rb[:, :, 2:2 + CWB], reads=["Rb"], writes=["outT"])
    S.finish("sp", ["outT"])
    return nc


def ffn_pvec(mix_norm_g, b_glu, ln1_g, ln1_b, ln2_g, ln2_b, conv_w, conv_b, flag):
    pv = np.zeros((128, NPB), np.float32)
    pv[:, PB_GMIX:PB_GMIX + 8] = mix_norm_g.reshape(8, 128).T
    pv[:, PB_BGLU:PB_BGLU + 2] = b_glu.reshape(2, 128).T
    pv[:, PB_LN1G:PB_LN1G + 8] = ln1_g.reshape(8, 128).T
    pv[:, PB_LN1B:PB_LN1B + 8] = ln1_b.reshape(8, 128).T
    pv[:, PB_LN2G:PB_LN2G + 8] = ln2_g.reshape(8, 128).T
    pv[:, PB_LN2B:PB_LN2B + 8] = ln2_b.reshape(8, 128).T
    pv[:, PB_CW:PB_CW + 144] = conv_w.reshape(3, 48, 128).transpose(2, 0, 1).reshape(128, 144)
    pv[:, PB_CB:PB_CB + 48] = conv_b.reshape(48, 128).T
    pv[:, PB_FLAG] = flag
    return pv


NCOLA = 1472
CQ01, CQ01S, CK01, CK01S, CQ2, CQ2S, CK2, CK2S, CV, CXA, CXB, CGA, CGB, CU = (
    0, 128, 256, 384, 512, 576, 640, 704, 768, 960, 1088, 1152, 1280, 1344)
PA_CW, PA_CB, PA_BR, PA_BI, PA_LAM, PA_D = 0, 8, 10, 12, 14, 16
NPA = 17
DILS = (1, 4, 16)
PI = float(np.pi)


def build_mixer():
    from contextlib import ExitStack
    nc = bass.Bass("TRN2", target_bir_lowering=False)
    S = Sync(nc)
    D = lambda name, shape: nc.dram_tensor(name, shape, F32, kind="ExternalInput").ap()
    hTd = D("hT", [1024, SEQ])
    w_a = D("w_a", [1024, NCOLA])
    pvA = D("pvA", [128, NPA])
    lru_w = D("lru_w", [128, 4, 128])
    cosd = D("cosT", [128, SEQ])
    sind = D("sinT", [128, SEQ])
    amask = D("amask", [128, 2, 512])
    s5A1 = D("s5A1", [128, 3, 64])
    s5B1 = D("s5B1", [128, 2, 64])
    s5A2 = D("s5A2", [128, 3, 4])
    s5C2 = D("s5C2", [128, 2, 64])
    s5B2 = D("s5B2", [128, 2, 64])
    msk1 = D("msk1", [128, 512])
    msk2 = D("msk2", [128, 4, 128])
    identd = D("ident", [128, 128])
    mixo = nc.dram_tensor("mixo", [384, SEQ], F32, kind="ExternalOutput").ap()
    gy8 = nc.dram_tensor("gy8", [128, 8, 512], F32, kind="ExternalOutput").ap()

    A = nc.alloc_sbuf_tensor
    Pm = nc.alloc_psum_tensor
    ps = [Pm(f"ps{i}", [128, 512], F32) for i in range(8)]
    psn = [f"ps{i}" for i in range(8)]

    def barrier():
        for e in S.eng:
            for pe in S.eng:
                if pe != e and S.seen[e][pe] < S.cnt[pe]:
                    S.eng[e].wait_ge(S.sem[pe], S.cnt[pe])
                    S.seen[e][pe] = S.cnt[pe]
            for si in range(S.ndma):
                if S.dcnt[si] > 0 and S.seen_dma[e].get(si, 0) < S.dcnt[si]:
                    S.eng[e].wait_ge(S.dsem[si], S.dcnt[si])
                    S.seen_dma[e][si] = S.dcnt[si]

    def tt(o, a, b, op, eng="dve"):
        S.op(eng, lambda e: e.tensor_tensor(out=o[0], in0=a[0], in1=b[0], op=op), reads=[a[1], b[1]], writes=[o[1]])

    def ts(o, a, s1, op, eng="dve"):
        rd = [a[1]]
        sc = s1
        if isinstance(s1, tuple):
            rd.append(s1[1])
            sc = s1[0]
        S.op(eng, lambda e: e.tensor_scalar(out=o[0], in0=a[0], scalar1=sc, scalar2=None, op0=op), reads=rd, writes=[o[1]])

    def stt(o, a, s1, b, op0, op1, eng="dve"):
        rd = [a[1], b[1]]
        sc = s1
        if isinstance(s1, tuple):
            rd.append(s1[1])
            sc = s1[0]
        S.op(eng, lambda e: e.scalar_tensor_tensor(out=o[0], in0=a[0], scalar=sc, in1=b[0], op0=op0, op1=op1), reads=rd, writes=[o[1]])

    def actf(o, a, func, scale=1.0, bias=None):
        rd = [a[1]]
        kw = {}
        sc = scale
        if isinstance(scale, tuple):
            rd.append(scale[1])
            sc = scale[0]
        if bias is not None:
            if isinstance(bias, tuple):
                rd.append(bias[1])
                kw["bias"] = bias[0]
            else:
                kw["bias"] = bias
        S.op("act", lambda e: e.activation(out=o[0], in_=a[0], func=func, scale=sc, **kw), reads=rd, writes=[o[1]])

    def mm(o, lhsT, rhs, start, stop):
        S.op("pe", lambda e: e.matmul(o[0], lhsT[0], rhs[0], start=start, stop=stop), reads=[lhsT[1], rhs[1]], writes=[o[1]])

    MUL, ADD, SUB = ALU.mult, ALU.add, ALU.subtract

    pv = A("pvA_sb", [128, NPA], F32)
    S.dma("sp", pv[:], pvA, writes=["pv"])
    hT = A("hT_sb", [128, 8, SEQ], BF16)
    hT3 = hTd.rearrange("(k p) w -> p k w", p=128)
    for k in range(8):
        for q4 in range(4):
            S.dma("pool", hT[:, k, q4 * 1024:(q4 + 1) * 1024], hT3[:, k, q4 * 1024:(q4 + 1) * 1024], writes=[f"hT{k}"])
    HT = [f"hT{k}" for k in range(8)]
    W = A("W_sb", [128, 8, NCOLA], BF16)
    for k in range(8):
        S.dma("pool", W[:, k, :], w_a[k * 128:(k + 1) * 128, :], writes=[f"W{k}"], max_dma_last_dim=4096)

    def proj(out_ps, col0, ncol, tok_ap_fn):
        for k in range(8):
            mm(out_ps, (W[:, k, col0:col0 + ncol], f"W{k}"), (tok_ap_fn(k), f"hT{k}"), k == 0, k == 7)

    with ExitStack() as es:
        def T(name, shape, dt=F32):
            t = es.enter_context(nc.sbuf_tensor(name, shape, dt))
            return t
        a1 = T("a1", [128, 3, 64]); b1 = T("b1", [128, 2, 64]); a2 = T("a2", [128, 3, 4])
        c2 = T("c2", [128, 2, 4, 16]); b2 = T("b2", [128, 2, 4, 16])
        m1 = T("m1", [128, 8, 64]); m2 = T("m2", [128, 4, 8, 16]); idt = T("idt", [128, 128])
        S.dma("sp", a1[:], s5A1, writes=["a1"]); S.dma("sp", b1[:], s5B1, writes=["b1"])
        S.dma("sp", a2[:], s5A2, writes=["a2"])
        S.dma("sp", c2[:], s5C2.rearrange("p a (b c) -> p a b c", c=16), writes=["c2"])
        S.dma("sp", b2[:], s5B2.rearrange("p a (b c) -> p a b c", c=16), writes=["b2"])
        S.dma("sp", m1[:], msk1.rearrange("p (a b) -> p a b", b=64), writes=["m1"])
        S.dma("sp", m2[:], msk2.rearrange("p a (b c) -> p a b c", c=16), writes=["m2"])
        S.dma("sp", idt[:], identd, writes=["idt"])

        WinR = T("WinR", [128, 8, 512], BF16); WinI = T("WinI", [128, 8, 512], BF16)
        WoM = T("WoM", [128, 4, 2, 9, 128], BF16)
        BD = T("BD", [128, 8, 128], BF16)
        ER = T("ER", [128, 4, 512]); EI = T("EI", [128, 4, 512])
        rho = T("rho", [128, 4])
        ub = T("ub", [128, SEQ], BF16)
        xs = T("xs", [128, 4, 2, 513], BF16)

        def chain(pre, Are, Aim, Ls, shape, tmp):
            def new(nm):
                t = tmp(f"{pre}_{nm}", shape)
                return (t[:], f"{pre}_{nm}")
            step = new("step"); actf(step, Ls, AF.Exp)
            dtr = new("dtr"); tt(dtr, step, Are, MUL)
            dti = new("dti"); tt(dti, step, Aim, MUL)
            mag = new("mag"); actf(mag, dtr, AF.Exp)
            sn = new("sn"); actf(sn, dti, AF.Sin, scale=1.0 / 16)
            cs = new("cs"); actf(cs, dti, AF.Sin, scale=1.0 / 16, bias=PI / 2)
            t1, t2, t3 = new("t1"), new("t2"), new("t3")
            for _ in range(4):
                tt(t1, sn, cs, MUL); tt(t2, cs, cs, MUL); tt(t3, sn, sn, MUL)
                ts(sn, t1, 2.0, MUL); tt(cs, t2, t3, SUB)
            abr = new("abr"); tt(abr, mag, cs, MUL)
            abi = new("abi"); tt(abi, mag, sn, MUL)
            zr = new("zr"); ts(zr, abr, -1.0, ADD)
            den = new("den"); tt(t1, Are, Are, MUL); tt(t2, Aim, Aim, MUL); tt(den, t1, t2, ADD)
            S.op("dve", lambda e: e.reciprocal(out=den[0], in_=den[0]), reads=[den[1]], writes=[den[1]])
            fr = new("fr"); tt(t1, zr, Are, MUL); tt(t2, abi, Aim, MUL); tt(t3, t1, t2, ADD); tt(fr, t3, den, MUL)
            fi = new("fi"); tt(t1, abi, Are, MUL); tt(t2, zr, Aim, MUL); tt(t3, t1, t2, SUB); tt(fi, t3, den, MUL)
            return dict(abr=abr, abi=abi, fr=fr, fi=fi, dtr=dtr, t1=t1, t2=t2)

        def cmul(o_r, o_i, ar, ai, br, bi, t1, t2):
            tt(t1, ar, br, MUL); tt(t2, ai, bi, MUL); tt(o_r, t1, t2, SUB)
            tt(t1, ar, bi, MUL); tt(t2, ai, br, MUL); tt(o_i, t1, t2, ADD)

        with ExitStack() as es2:
            def T2(name, shape, dt=F32):
                return es2.enter_context(nc.sbuf_tensor(name, shape, dt))
            c1 = chain("L1", (a1[:, 0, :], "a1"), (a1[:, 1, :], "a1"), (a1[:, 2, :], "a1"), [128, 64], T2)
            CinR = T2("CinR", [128, 8, 64]); CinI = T2("CinI", [128, 8, 64])
            cmul((CinR[:, 0, :], "CinR0"), (CinI[:, 0, :], "CinI0"), c1["fr"], c1["fi"], (b1[:, 0, :], "b1"), (b1[:, 1, :], "b1"),
                 c1["t1"], c1["t2"])
            for j in range(7):
                cmul((CinR[:, j + 1, :], f"CinR{j + 1}"), (CinI[:, j + 1, :], f"CinI{j + 1}"), c1["abr"], c1["abi"],
                     (CinR[:, j, :], f"CinR{j}"), (CinI[:, j, :], f"CinI{j}"), c1["t1"], c1["t2"])
            for s in range(8):
                for (Wn, Cn, nm) in ((WinR, CinR, "R"), (WinI, CinI, "I")):
                    tt((Wn[:, s, :].rearrange("p (a b) -> p a b", b=64), f"Win{nm}{s}"),
                       (Cn[:, 7 - s, :].unsqueeze(1).to_broadcast([128, 8, 64]), f"Cin{nm}{7 - s}"), (m1[:], "m1"), MUL)
            c2c = chain("L2", (a2[:, 0, :], "a2"), (a2[:, 1, :], "a2"), (a2[:, 2, :], "a2"), [128, 4], T2)
            apR = T2("apR", [128, 9, 4]); apI = T2("apI", [128, 9, 4])
            S.op("dve", lambda e: e.memset(apR[:, 0, :], 1.0), writes=["apR0"])
            S.op("dve", lambda e: e.memset(apI[:, 0, :], 0.0), writes=["apI0"])
            for j in range(8):
                cmul((apR[:, j + 1, :], f"apR{j + 1}"), (apI[:, j + 1, :], f"apI{j + 1}"), c2c["abr"], c2c["abi"],
                     (apR[:, j, :], f"apR{j}"), (apI[:, j, :], f"apI{j}"), c2c["t1"], c2c["t2"])
            q1 = T2("q1", [128, 4, 16]); q2 = T2("q2", [128, 4, 16]); q3 = T2("q3", [128, 4, 16])
            woR = T2("woR", [128, 4, 16]); woI = T2("woI", [128, 4, 16])
            Q1, Q2, Q3 = (q1[:], "q1"), (q2[:], "q2"), (q3[:], "q3")
            cR, cI = (c2[:, 0], "c2"), (c2[:, 1], "c2")
            for j in range(9):
                bR = (apR[:, j, :].unsqueeze(2).to_broadcast([128, 4, 16]), f"apR{j}")
                bI = (apI[:, j, :].unsqueeze(2).to_broadcast([128, 4, 16]), f"apI{j}")
                tt(Q1, cR, bR, MUL); tt(Q2, cI, bI, MUL); tt((woR[:], "woR"), Q1, Q2, SUB)
                tt(Q1, cR, bI, MUL); tt(Q2, cI, bR, MUL); tt(Q3, Q1, Q2, ADD); ts((woI[:], "woI"), Q3, -1.0, MUL)
                for Pp in range(4):
                    for comp, wo in enumerate((woR, woI)):
                        tt((WoM[:, Pp, comp, j, :].rearrange("p (a b) -> p a b", b=16), f"WoM{Pp}_{comp}_{j}"),
                           (wo[:, Pp, :].unsqueeze(1).to_broadcast([128, 8, 16]), "woR" if comp == 0 else "woI"),
                           (m2[:, Pp], "m2"), MUL)
            bbR = T2("bbR", [128, 4, 16]); bbI = T2("bbI", [128, 4, 16])
            fRb = (c2c["fr"][0].unsqueeze(2).to_broadcast([128, 4, 16]), c2c["fr"][1])
            fIb = (c2c["fi"][0].unsqueeze(2).to_broadcast([128, 4, 16]), c2c["fi"][1])
            cmul((bbR[:], "bbR"), (bbI[:], "bbI"), fRb, fIb, (b2[:, 0], "b2"), (b2[:, 1], "b2"), Q1, Q2)
            BBm = T2("BBm", [128, 4, 2, 128], BF16)
            for Pp in range(4):
                for comp, bb in enumerate((bbR, bbI)):
                    tt((BBm[:, Pp, comp, :].rearrange("p (a b) -> p a b", b=16), f"BBm{Pp}_{comp}"),
                       (bb[:, Pp, :].unsqueeze(1).to_broadcast([128, 8, 16]), "bbR" if comp == 0 else "bbI"),
                       (m2[:, Pp], "m2"), MUL)
            for d in range(8):
                pb, pbn = ps[d % 2], psn[d % 2]
                i = 0
                for Pp in range(4):
                    for comp in range(2):
                        mm((pb[:, 0:128], pbn), (BBm[:, Pp, comp, :], f"BBm{Pp}_{comp}"), (WoM[:, Pp, comp, d, :], f"WoM{Pp}_{comp}_{d}"),
                           i == 0, i == 7)
                        i += 1
                if d == 0:
                    stt((BD[:, 0, :], "BD0"), (idt[:], "idt"), (pv[:, PA_D:PA_D + 1], "pv"), (pb[:, 0:128], pbn), MUL, ADD)
                else:
                    actf((BD[:, d, :], f"BD{d}"), (pb[:, 0:128], pbn), AF.Copy)
            actf((rho[:], "rho"), c2c["dtr"], AF.Exp, scale=8.0)
            m8 = T2("m8", [128, 4]); actf((m8[:], "m8"), c2c["dtr"], AF.Exp, scale=-8.0)
            pr = T2("pr", [128, 4]); pi_ = T2("pi_", [128, 4]); pt1 = T2("pt1", [128, 4]); pt2 = T2("pt2", [128, 4]); pt3 = T2("pt3", [128, 4])
            PR, PIm, PT1, PT2, PT3 = (pr[:], "pr"), (pi_[:], "pi_"), (pt1[:], "pt1"), (pt2[:], "pt2"), (pt3[:], "pt3")
            tt(PR, (apR[:, 8, :], "apR8"), (m8[:], "m8"), MUL); tt(PIm, (apI[:, 8, :], "apI8"), (m8[:], "m8"), MUL)
            S.op("dve", lambda e: e.memset(ER[:, :, 0:1], 1.0), writes=["ER"])
            S.op("dve", lambda e: e.memset(EI[:, :, 0:1], 0.0), writes=["EI"])
            e1 = T2("e1", [128, 4, 256]); e2 = T2("e2", [128, 4, 256])
            for kk in range(9):
                ln = 1 << kk
                prb = (pr[:].unsqueeze(2).to_broadcast([128, 4, ln]), "pr")
                pib = (pi_[:].unsqueeze(2).to_broadcast([128, 4, ln]), "pi_")
                E1, E2 = (e1[:, :, 0:ln], "e1"), (e2[:, :, 0:ln], "e2")
                tt(E1, (ER[:, :, 0:ln], "ER"), prb, MUL); tt(E2, (EI[:, :, 0:ln], "EI"), pib, MUL)
                tt((ER[:, :, ln:2 * ln], "ER"), E1, E2, SUB)
                tt(E1, (ER[:, :, 0:ln], "ER"), pib, MUL); tt(E2, (EI[:, :, 0:ln], "EI"), prb, MUL)
                tt((EI[:, :, ln:2 * ln], "EI"), E1, E2, ADD)
                if kk < 8:
                    tt(PT1, PR, PR, MUL); tt(PT2, PIm, PIm, MUL); tt(PT3, PR, PIm, MUL)
                    tt(PR, PT1, PT2, SUB); ts(PIm, PT3, 2.0, MUL)
        for cch in range(8):
            pu, pun = ps[2 + cch % 2], psn[2 + cch % 2]
            proj((pu[:, :], pun), CU, 128, lambda k, cch=cch: hT[:, k, cch * 512:(cch + 1) * 512])
            actf((ub[:, cch * 512:(cch + 1) * 512], "ub"), (pu[:, :], pun), AF.Copy)
        S.op("dve", lambda e: e.memset(xs[:, :, :, 0:1], 0.0), writes=["xs"])
        vR = T("vR", [128, 512]); vI = T("vI", [128, 512]); u1 = T("u1", [128, 512]); u2 = T("u2", [128, 512])
        wR = T("wR", [128, 512]); wI = T("wI", [128, 512]); zR = T("zR", [128, 512]); zI = T("zI", [128, 512]); rT = T("rT", [128, 512])
        VR, VI, U1, U2, WR, WI, ZR, ZI, RT = ((vR[:], "vR"), (vI[:], "vI"), (u1[:], "u1"), (u2[:], "u2"), (wR[:], "wR"),
                                              (wI[:], "wI"), (zR[:], "zR"), (zI[:], "zI"), (rT[:], "rT"))
        for Pp in range(4):
            for comp, (Wn, nm, Vv) in enumerate(((WinR, "R", VR), (WinI, "I", VI))):
                pvv, pvn = ps[4 + comp], psn[4 + comp]
                for s in range(8):
                    mm((pvv[:, :], pvn), (Wn[:, s, Pp * 128:(Pp + 1) * 128], f"Win{nm}{s}"), (ub[:, s::8], "ub"), s == 0, s == 7)
                actf(Vv, (pvv[:, :], pvn), AF.Copy)
            er, ei = (ER[:, Pp, :], "ER"), (EI[:, Pp, :], "EI")
            tt(U1, VR, er, MUL); tt(U2, VI, ei, MUL, eng="pool"); tt(WR, U1, U2, ADD)
            tt(U1, VI, er, MUL); tt(U2, VR, ei, MUL, eng="pool"); tt(WI, U1, U2, SUB)
            S.op("dve", lambda e, Pp=Pp: e.tensor_copy(out=rT[:], in_=rho[:, Pp:Pp + 1].to_broadcast([128, 512])), reads=["rho"], writes=["rT"])
            for (Zz, Ww) in ((ZR, WR), (ZI, WI)):
                S.op("dve", lambda e, Zz=Zz, Ww=Ww: e.tensor_tensor_scan(out=Zz[0], data0=rT[:], data1=Ww[0], initial=0.0,
                                                                          op0=MUL, op1=ADD), reads=["rT", Ww[1]], writes=[Zz[1]])
            tt(U1, ZR, er, MUL); tt(U2, ZI, ei, MUL, eng="pool"); tt((xs[:, Pp, 0, 1:513], "xs"), U1, U2, SUB)
            tt(U1, ZR, ei, MUL); tt(U2, ZI, er, MUL, eng="pool"); tt((xs[:, Pp, 1, 1:513], "xs"), U1, U2, ADD)
        go = T("go", [128, 2, 512])
        for t in range(8):
            py, pyn = ps[6 + t % 2], psn[6 + t % 2]
            n_mm = (t + 1) + 8
            i = 0
            for s in range(t + 1):
                mm((py[:, :], pyn), (BD[:, t - s, :], f"BD{t - s}"), (ub[:, s::8], "ub"), i == 0, i == n_mm - 1)
                i += 1
            for Pp in range(4):
                for comp in range(2):
                    mm((py[:, :], pyn), (WoM[:, Pp, comp, t + 1, :], f"WoM{Pp}_{comp}_{t + 1}"), (xs[:, Pp, comp, 0:512], "xs"),
                       i == 0, i == n_mm - 1)
                    i += 1
            actf((go[:, t % 2, :], f"go{t % 2}"), (py[:, :], pyn), AF.Gelu_apprx_tanh)
            S.dma("sp", gy8[:, t, :], go[:, t % 2, :], reads=[f"go{t % 2}"], writes=["gy8"])
        barrier()

    with ExitStack() as es:
        def T(name, shape, dt=F32):
            return es.enter_context(nc.sbuf_tensor(name, shape, dt))
        lw = T("lw", [128, 4, 128], BF16)
        S.dma("pool", lw[:], lru_w, writes=["lw"])
        sc = T("sc", [128, 2]); sc2 = T("sc2", [128, 2])
        SC, SC2 = (sc[:], "sc"), (sc2[:], "sc2")
        actf(SC, (pv[:, PA_LAM:PA_LAM + 2], "pv"), AF.Exp, scale=-1.0)
        actf(SC, SC, AF.Ln, scale=1.0, bias=1.0)
        ts(SC, SC, -8.0, MUL); ts(SC2, SC, 2.0, MUL)
        tiles = []
        for ti, (nr, xcol, gcol, row0) in enumerate(((128, CXA, CGA, 192), (64, CXB, CGB, 320))):
            d = dict(nr=nr, xcol=xcol, gcol=gcol, row0=row0, ti=ti)
            d["xr"] = T(f"lru_xr{ti}", [128, 515]); d["hs"] = T(f"lru_hs{ti}", [128, 2, 512])
            for nm in ("xc", "r", "ii", "a", "a2", "u", "gg", "ob"):
                d[nm] = T(f"lru_{nm}{ti}", [128, 512])
            d["xcb"] = T(f"lru_xcb{ti}", [128, 512], BF16)
            S.op("dve", lambda e, d=d: e.memset(d["xr"][:, 0:3], 0.0), writes=[f"xr{ti}"])
            tiles.append(d)
        for cch in range(8):
            csl = slice(cch * 512, (cch + 1) * 512)
            for d in tiles:
                ti, nr = d["ti"], d["nr"]
                px, pxn = ps[ti], psn[ti]; pg, pgn = ps[2 + ti], psn[2 + ti]
                pr_, prn = ps[4 + ti], psn[4 + ti]; pi2, pin = ps[6 + ti], psn[6 + ti]
                proj((px[0:nr, :], pxn), d["xcol"], nr, lambda k: hT[:, k, csl])
                proj((pg[0:nr, :], pgn), d["gcol"], nr, lambda k: hT[:, k, csl])
                xr, xrn = d["xr"], f"xr{ti}"
                XC = (d["xc"][0:nr, :], f"xc{ti}")
                pcol = lambda base: (pv[0:nr, base + ti:base + ti + 1], "pv")
                cwc = lambda tap: (pv[0:nr, PA_CW + tap * 2 + ti:PA_CW + tap * 2 + ti + 1], "pv")
                actf((xr[0:nr, 3:515], xrn), (px[0:nr, :], pxn), AF.Copy)
                actf(XC, (px[0:nr, :], pxn), AF.Identity, scale=cwc(3), bias=pcol(PA_CB))
                stt(XC, (xr[0:nr, 2:514], xrn), cwc(2), XC, MUL, ADD)
                stt(XC, (xr[0:nr, 1:513], xrn), cwc(1), XC, MUL, ADD)
                stt(XC, (xr[0:nr, 0:512], xrn), cwc(0), XC, MUL, ADD)
                S.op("dve", lambda e, xr=xr, nr=nr: e.tensor_copy(out=xr[0:nr, 0:3], in_=xr[0:nr, 512:515]), reads=[xrn], writes=[xrn])
                XCB = (d["xcb"][0:nr, :], f"xcb{ti}")
                actf(XCB, XC, AF.Copy)
                mm((pr_[0:nr, :], prn), (lw[0:nr, ti, 0:nr], "lw"), XCB, True, True)
                mm((pi2[0:nr, :], pin), (lw[0:nr, 2 + ti, 0:nr], "lw"), XCB, True, True)
                Rr, II, Aa, A2, Uu, GG, OB = [(d[nm][0:nr, :], f"{nm}{ti}") for nm in ("r", "ii", "a", "a2", "u", "gg", "ob")]
                actf(Rr, (pr_[0:nr, :], prn), AF.Sigmoid, bias=pcol(PA_BR))
                actf(II, (pi2[0:nr, :], pin), AF.Sigmoid, bias=pcol(PA_BI))
                actf(Aa, Rr, AF.Exp, scale=(sc[0:nr, ti:ti + 1], "sc"))
                actf(A2, Rr, AF.Exp, scale=(sc2[0:nr, ti:ti + 1], "sc2"))
                actf(A2, A2, AF.Sqrt, scale=-1.0, bias=1.0)
                tt(Uu, II, XC, MUL, eng="pool")
                tt(Uu, Uu, A2, MUL)
                hs = d["hs"]
                cur = (hs[0:nr, cch % 2, :], f"hs{ti}_{cch % 2}")
                if cch == 0:
                    S.op("dve", lambda e, cur=cur, Aa=Aa, Uu=Uu: e.tensor_tensor_scan(out=cur[0], data0=Aa[0], data1=Uu[0], initial=0.0,
                                                                                         op0=MUL, op1=ADD),
                         reads=[Aa[1], Uu[1]], writes=[cur[1]])
                else:
                    prev = hs[0:nr, (cch - 1) % 2, 511:512]
                    S.op("dve", lambda e, cur=cur, Aa=Aa, Uu=Uu, prev=prev: e.tensor_tensor_scan(out=cur[0], data0=Aa[0], data1=Uu[0],
                                                                                                    initial=prev, op0=MUL, op1=ADD),
                         reads=[Aa[1], Uu[1], f"hs{ti}_{(cch - 1) % 2}"], writes=[cur[1]])
                actf(GG, (pg[0:nr, :], pgn), AF.Gelu_apprx_tanh)
                tt(OB, cur, GG, MUL, eng="pool")
                S.dma("sp", mixo[d["row0"]:d["row0"] + nr, csl], OB[0], reads=[OB[1]], writes=["mixo"])
        barrier()

    with ExitStack() as es:
        def T(name, shape, dt=F32):
            return es.enter_context(nc.sbuf_tensor(name, shape, dt))
        qT01 = T("qT01", [128, SEQ], BF16); kT01 = T("kT01", [128, SEQ], BF16)
        qT2 = T("qT2", [128, SEQ], BF16); kT2 = T("kT2", [128, SEQ], BF16)
        Vsb = T("Vsb", [128, 3, 32 * 192], BF16)
        am = T("am", [128, 2, 512])
        S.dma("sp", am[:], amask, writes=["am"])
        onesb = T("onesb", [128, 64], BF16)
        S.op("dve", lambda e: e.memset(onesb[:], 1.0), writes=["onesb"])
        with ExitStack() as es2:
            def T2(name, shape, dt=F32):
                return es2.enter_context(nc.sbuf_tensor(name, shape, dt))
            cs_sb = T2("cs_sb", [128, 2, 512]); sn_sb = T2("sn_sb", [128, 2, 512])
            r1 = T2("r1", [128, 2, 512]); r2 = T2("r2", [128, 2, 512])
            gi_all = 0
            for cch in range(8):
                csl = slice(cch * 512, (cch + 1) * 512)
                b = cch % 2
                S.dma("sp", cs_sb[:, b, :], cosd[:, csl], writes=[f"cs{b}"])
                S.dma("sp", sn_sb[:, b, :], sind[:, csl], writes=[f"sn{b}"])
                for (c0, c0s, M, dst, dname) in ((CQ01, CQ01S, 128, qT01, "qT01"), (CK01, CK01S, 128, kT01, "kT01"),
                                                 (CQ2, CQ2S, 64, qT2, "qT2"), (CK2, CK2S, 64, kT2, "kT2")):
                    g2 = gi_all % 2
                    gi_all += 1
                    p1, p1n = ps[2 * g2], psn[2 * g2]
                    p2, p2n = ps[2 * g2 + 1], psn[2 * g2 + 1]
                    proj((p1[0:M, :], p1n), c0, M, lambda k: hT[:, k, csl])
                    proj((p2[0:M, :], p2n), c0s, M, lambda k: hT[:, k, csl])
                    R1, R2 = (r1[0:M, g2, :], f"r1_{g2}"), (r2[0:M, g2, :], f"r2_{g2}")
                    tt(R1, (p1[0:M, :], p1n), (cs_sb[0:M, b, :], f"cs{b}"), MUL)
                    tt(R2, (p2[0:M, :], p2n), (sn_sb[0:M, b, :], f"sn{b}"), MUL)
                    tt((dst[0:M, csl], dname), R1, R2, ADD, eng="pool")
            it = 0
            for di, dl in enumerate(DILS):
                nb = 32 // dl
                for c in range(dl):
                    for n in range(0, nb, 2):
                        pvp, pvpn = ps[4 + it % 2], psn[4 + it % 2]
                        it += 1
                        for half in range(2):
                            base = c + (n + half) * 128 * dl
                            for k in range(8):
                                mm((pvp[:, half * 192:(half + 1) * 192], pvpn), (hT[:, k, base:base + 127 * dl + 1:dl], f"hT{k}"),
                                   (W[:, k, CV:CV + 192], f"W{k}"), k == 0, k == 7)
                        blk = c * nb + n
                        actf((Vsb[:, di, blk * 192:(blk + 2) * 192], "Vsb"), (pvp[:, 0:384], pvpn), AF.Copy)
        accN = T("accN", [128, SEQ]); accD = T("accD", [128, SEQ])
        Eb = T("Eb", [128, 2, 512]); Pb = T("Pb", [128, 2, 512], BF16)
        for h in range(3):
            if h < 2:
                qh, kh, qn, kn = qT01[64 * h:64 * h + 64, :], kT01[64 * h:64 * h + 64, :], "qT01", "kT01"
            else:
                qh, kh, qn, kn = qT2[0:64, :], kT2[0:64, :], "qT2", "kT2"
            it = 0
            for di, dl in enumerate(DILS):
                nb = 32 // dl
                for c in range(dl):
                    for n in range(0, nb, 2):
                        b = it % 2
                        it += 1
                        pS, pSn = ps[b], psn[b]
                        pN, pNn = ps[2 + b], psn[2 + b]
                        pD, pDn = ps[4 + b], psn[4 + b]

                        def tok(nn, c=c, dl=dl):
                            base = c + nn * 128 * dl
                            return slice(base, base + 127 * dl + 1, dl)
                        slots = [(n, n - 1 if n > 0 else 0), (n, n), (n + 1, n), (n + 1, n + 1)]
                        for si, (qb, kb) in enumerate(slots):
                            mm((pS[:, si * 128:(si + 1) * 128], pSn), (kh[:, tok(kb)], kn), (qh[:, tok(qb)], qn), True, True)
                        actf((Eb[:, b, :], f"Eb{b}"), (pS[:, :], pSn), AF.Exp, scale=0.125)
                        tt((Pb[:, b, :], f"Pb{b}"), (Eb[:, b, :], f"Eb{b}"), (am[:, 1 if n == 0 else 0, :], "am"), MUL, eng="pool")
                        for qi in range(2):
                            for j2, si in enumerate((2 * qi, 2 * qi + 1)):
                                blk = c * nb + slots[si][1]
                                mm((pN[0:64, qi * 128:(qi + 1) * 128], pNn), (Vsb[:, di, blk * 192 + h * 64:blk * 192 + (h + 1) * 64], "Vsb"),
                                   (Pb[:, b, si * 128:(si + 1) * 128], f"Pb{b}"), j2 == 0, j2 == 1)
                            for j2, si in enumerate((2 * qi, 2 * qi + 1)):
                                mm((pD[0:64, qi * 128:(qi + 1) * 128], pDn), (onesb[:, :], "onesb"),
                                   (Pb[:, b, si * 128:(si + 1) * 128], f"Pb{b}"), j2 == 0, j2 == 1)
                        base = c + n * 128 * dl
                        vsl = slice(base, base + 255 * dl + 1, dl)
                        AN, AD = (accN[0:64, vsl], "accN"), (accD[0:64, vsl], "accD")
                        if di == 0:
                            actf(AN, (pN[0:64, 0:256], pNn), AF.Copy)
                            actf(AD, (pD[0:64, 0:256], pDn), AF.Copy)
                        else:
                            tt(AN, AN, (pN[0:64, 0:256], pNn), ADD)
                            tt(AD, AD, (pD[0:64, 0:256], pDn), ADD)
            S.op("dve", lambda e: e.reciprocal(out=accD[0:64, :], in_=accD[0:64, :]), reads=["accD"], writes=["accD"])
            tt((accN[0:64, :], "accN"), (accN[0:64, :], "accN"), (accD[0:64, :], "accD"), MUL, eng="pool")
            S.dma("sp", mixo[h * 64:(h + 1) * 64, :], accN[0:64, :], reads=["accN"], writes=["mixo"])
        barrier()
    S.finish("sp", ["mixo", "gy8"])
    return nc


def _swap(w):
    return np.concatenate([w[:, 32:], w[:, :32]], axis=1)


_CONSTS = {}


def mixer_consts():
    if _CONSTS:
        return _CONSTS
    half = 32
    pos = np.arange(SEQ, dtype=np.float32)
    inv = (np.float32(10000.0) ** (-np.arange(half, dtype=np.float32) * np.float32(2.0) / np.float32(64))).astype(np.float32)
    ang = (pos[:, None] * inv[None, :]).astype(np.float32)
    cos = np.cos(ang).astype(np.float32).T
    sin = np.sin(ang).astype(np.float32).T
    c64 = np.concatenate([cos, cos], axis=0)
    s64 = np.concatenate([-sin, sin], axis=0)
    _CONSTS["cosT"] = np.ascontiguousarray(np.concatenate([c64, c64], axis=0))
    _CONSTS["sinT"] = np.ascontiguousarray(np.concatenate([s64, s64], axis=0))
    ki = np.arange(128)[:, None]
    qi = np.arange(128)[None, :]
    mcur = (qi >= ki).astype(np.float32)
    mprev = (qi <= ki).astype(np.float32)
    z = np.zeros((128, 128), np.float32)
    am = np.stack([np.concatenate([mprev, mcur, mprev, mcur], axis=1), np.concatenate([z, mcur, mprev, mcur], axis=1)], axis=1)
    _CONSTS["amask"] = np.ascontiguousarray(am)
    r = np.arange(128)
    m1 = np.zeros((128, 8, 64), np.float32)
    for grp in range(8):
        m1[r // 16 == grp, grp, :] = 1.0
    _CONSTS["msk1"] = m1.reshape(128, 512)
    m2 = np.zeros((128, 4, 8, 16), np.float32)
    for Pp in range(4):
        for g2 in range(2):
            m2[g2 * 64:(g2 + 1) * 64, Pp, 2 * Pp + g2, :] = 1.0
    _CONSTS["msk2"] = m2.reshape(128, 4, 128)
    _CONSTS["ident"] = np.eye(128, dtype=np.float32)
    return _CONSTS


def mixer_inputs(P, hT_b, g):
    w_in = P["w_in"]
    QO, KO, VO, XO, GO, UO = 0, 384, 768, 1152, 1536, 1920
    hd = [3 * g, 3 * g + 1, 3 * g + 2]
    hc = lambda off, h: w_in[:, off + 64 * h:off + 64 * h + 64]
    cols = [hc(QO, hd[0]), hc(QO, hd[1]), _swap(hc(QO, hd[0])), _swap(hc(QO, hd[1])),
            hc(KO, hd[0]), hc(KO, hd[1]), _swap(hc(KO, hd[0])), _swap(hc(KO, hd[1])),
            hc(QO, hd[2]), _swap(hc(QO, hd[2])), hc(KO, hd[2]), _swap(hc(KO, hd[2])),
            w_in[:, VO + 192 * g:VO + 192 * g + 192],
            w_in[:, XO + 192 * g:XO + 192 * g + 192],
            w_in[:, GO + 192 * g:GO + 192 * g + 192],
            w_in[:, UO + 128 * g:UO + 128 * g + 128]]
    w_a = np.ascontiguousarray(np.concatenate(cols, axis=1))
    assert w_a.shape == (1024, NCOLA)
    pvA = np.zeros((128, NPA), np.float32)
    chA = slice(192 * g, 192 * g + 128)
    chB = slice(192 * g + 128, 192 * g + 192)
    for tap in range(4):
        pvA[:, PA_CW + tap * 2 + 0] = P["lru_conv_w"][tap, chA]
        pvA[:64, PA_CW + tap * 2 + 1] = P["lru_conv_w"][tap, chB]
    for base, name in ((PA_CB, "lru_conv_b"), (PA_BR, "lru_br"), (PA_BI, "lru_bi"), (PA_LAM, "lru_lambda")):
        pvA[:, base + 0] = P[name][chA]
        pvA[:64, base + 1] = P[name][chB]
    G0 = 8 * g
    pvA[:, PA_D] = P["s5_d"][G0:G0 + 8].reshape(128)
    lw = np.zeros((128, 4, 128), np.float32)
    for i, nm in enumerate(("lru_wr", "lru_wi")):
        lw[0:64, 2 * i, 0:64] = P[nm][hd[0]]
        lw[64:128, 2 * i, 64:128] = P[nm][hd[1]]
        lw[0:64, 2 * i + 1, 0:64] = P[nm][hd[2]]
    are, aim, ls = P["s5_a_re"][G0:G0 + 8], P["s5_a_im"][G0:G0 + 8], P["s5_log_step"][G0:G0 + 8]
    bre, bim = P["s5_b_re"][G0:G0 + 8], P["s5_b_im"][G0:G0 + 8]
    cre, cim = P["s5_c_re"][G0:G0 + 8], P["s5_c_im"][G0:G0 + 8]
    A1 = np.stack([np.repeat(are, 16, axis=0), np.repeat(aim, 16, axis=0),
                   np.repeat(np.broadcast_to(ls[:, None], (8, 64)), 16, axis=0)], axis=1)
    B1 = np.stack([bre.transpose(0, 2, 1).reshape(128, 64), bim.transpose(0, 2, 1).reshape(128, 64)], axis=1)
    l2 = lambda x: x.reshape(4, 2, 64).transpose(1, 2, 0).reshape(128, 4)
    A2 = np.stack([l2(are), l2(aim), l2(np.broadcast_to(ls[:, None], (8, 64)))], axis=1)
    l2c = lambda x: x.reshape(4, 2, 64, 16).transpose(1, 2, 0, 3).reshape(128, 64)
    C2 = np.stack([l2c(cre.transpose(0, 2, 1)), l2c(cim.transpose(0, 2, 1))], axis=1)
    B2 = np.stack([l2c(bre), l2c(bim)], axis=1)
    cs = mixer_consts()
    d = {"hT": np.ascontiguousarray(hT_b), "w_a": w_a, "pvA": pvA, "lru_w": lw,
         "s5A1": np.ascontiguousarray(A1, dtype=np.float32), "s5B1": np.ascontiguousarray(B1, dtype=np.float32),
         "s5A2": np.ascontiguousarray(A2, dtype=np.float32), "s5C2": np.ascontiguousarray(C2, dtype=np.float32),
         "s5B2": np.ascontiguousarray(B2, dtype=np.float32)}
    d.update(cs)
    return d


def mixer_unpack(res_pair):
    attn, lru, gy = [], [], []
    for r in res_pair:
        attn.append(r["mixo"][0:192])
        lru.append(r["mixo"][192:384])
        gy.append(r["gy8"].transpose(0, 2, 1).reshape(128, SEQ))
    return np.concatenate(attn + lru + gy, axis=0)


def _ffn_inputs(P, hT_b, mixT_b, half):
    t0 = half * TOKB
    hT = np.zeros((1024, TOKB + 2), np.float32)
    mT = np.zeros((1024, TOKB + 2), np.float32)
    hT[:, 2:] = hT_b[:, t0:t0 + TOKB]
    mT[:, 2:] = mixT_b[:, t0:t0 + TOKB]
    if half == 1:
        hT[:, :2] = hT_b[:, t0 - 2:t0]
        mT[:, :2] = mixT_b[:, t0 - 2:t0]
    pv = ffn_pvec(P["mix_norm_g"], P["s5_b_glu"], P["ln1_g"], P["ln1_b"], P["ln2_g"], P["ln2_b"],
                  P["ffn_conv_w"], P["ffn_conv_b"], float(half))
    return {"hT": hT, "mixT": mT, "w_glu": np.ascontiguousarray(P["s5_w_glu"]), "w_out": np.ascontiguousarray(P["w_out"]),
            "w_up": np.ascontiguousarray(P["w_up"]), "w_down": np.ascontiguousarray(P["w_down"]), "pvec": pv}


def kernel(**inputs):
    inputs = {k: np.asarray(v, dtype=np.float32) for k, v in inputs.items()}
    x = inputs["x"]
    hT_all = [np.ascontiguousarray(x[b].T) for b in range(BATCH)]
    cores = list(range(N_CORES))
    for l in range(DEPTH):
        P = {k: v[l] for k, v in inputs.items() if k != "x"}
        ncA = build_mixer()
        resA = run_bass_kernel_spmd(ncA, [mixer_inputs(P, hT_all[c // 2], c % 2) for c in cores], core_ids=cores)
        mixT = [mixer_unpack(resA.results[2 * b:2 * b + 2]) for b in range(BATCH)]
        ncB = build_ffn()
        resB = run_bass_kernel_spmd(ncB, [_ffn_inputs(P, hT_all[c // 2], mixT[c // 2], c % 2) for c in cores], core_ids=cores)
        hT_all = [np.concatenate([resB.results[2 * b]["outT"], resB.results[2 * b + 1]["outT"]], axis=1) for b in range(BATCH)]
    return np.ascontiguousarray(np.stack([h.T for h in hT_all], axis=0)).astype(np.float32)
```
